# Optimizing a Trainium2 kernel written in Bass

```python
import jax, jax.numpy as jnp
from jax import lax
import numpy as np

D_MODEL = 2048
BATCH = 4
SEQ = 2048
DEPTH = 2

MEM_LEN = 256
HEAD_DIM = 128
CHUNK = 128
GMLP_WIDTH = D_MODEL // 2
POOL_WIDTH = D_MODEL // 4
CONV_WIDTH = D_MODEL - GMLP_WIDTH - POOL_WIDTH
MIX_WIDTH = GMLP_WIDTH + POOL_WIDTH + CONV_WIDTH
GMLP_HEADS = GMLP_WIDTH // HEAD_DIM
POOL_WINDOWS = (2, 4, 8, 16)
POOL_GROUPS = len(POOL_WINDOWS)
POOL_GROUP_WIDTH = POOL_WIDTH // POOL_GROUPS
MAX_WINDOW = max(POOL_WINDOWS)
CONV_K = 31
IN_COLS = 2 * GMLP_WIDTH + POOL_WIDTH + 2 * CONV_WIDTH
XATTN_HEADS = 4
XATTN_HEAD_DIM = D_MODEL // XATTN_HEADS
D_FF = 4 * D_MODEL
RMS_EPS = 1e-6
LN_EPS = 1e-5

kernel_name = "hybrid_gmlp_pool_conformer_block"


def rms_norm(x, g):
    xf = x.astype(jnp.float32)
    y = xf * lax.rsqrt(jnp.mean(xf * xf, axis=-1, keepdims=True) + RMS_EPS)
    return (y * g.astype(jnp.float32)).astype(x.dtype)


def layer_norm(x, g, b=None):
    xf = x.astype(jnp.float32)
    mu = jnp.mean(xf, axis=-1, keepdims=True)
    var = jnp.mean(jnp.square(xf - mu), axis=-1, keepdims=True)
    y = (xf - mu) * lax.rsqrt(var + LN_EPS) * g.astype(jnp.float32)
    if b is not None:
        y = y + b.astype(jnp.float32)
    return y.astype(x.dtype)


def spatial_gating(u, v, g_v, w_s, b_s):
    B, S, _ = u.shape
    u = u.reshape(B, S, GMLP_HEADS, HEAD_DIM)
    v = layer_norm(v.reshape(B, S, GMLP_HEADS, HEAD_DIM), g_v)
    mask = jnp.tril(jnp.ones((CHUNK, CHUNK), w_s.dtype))
    vc = v.reshape(B, S // CHUNK, CHUNK, GMLP_HEADS, HEAD_DIM)
    mixed = jnp.einsum('hts,bcshd->bcthd', w_s * mask, vc) + b_s.T[None, None, :, :, None]
    return (u * mixed.reshape(B, S, GMLP_HEADS, HEAD_DIM)).reshape(B, S, GMLP_WIDTH)


def multiscale_pool(p, w_pool, s_pool):
    B, S, _ = p.shape
    pf = p.astype(jnp.float32).reshape(B, S, POOL_GROUPS, POOL_GROUP_WIDTH)
    cs = jnp.cumsum(pf, axis=1)
    cs = jnp.pad(cs, ((0, 0), (MAX_WINDOW, 0), (0, 0), (0, 0)))
    pos = jnp.arange(S, dtype=jnp.float32)
    means = []
    for g, w in enumerate(POOL_WINDOWS):
        win = cs[:, MAX_WINDOW:, g] - cs[:, MAX_WINDOW - w:MAX_WINDOW - w + S, g]
        cnt = jnp.minimum(pos + 1.0, float(w))[None, :, None]
        means.append(win / cnt)
    pooled = jnp.stack(means, axis=2) - pf
    out = jnp.einsum('bsgc,gcd->bsgd', pooled, w_pool.astype(jnp.float32))
    out = out * s_pool.astype(jnp.float32).reshape(POOL_GROUPS, POOL_GROUP_WIDTH)
    return out.reshape(B, S, POOL_WIDTH).astype(p.dtype)


def conformer_conv(c_val, c_gate, w_dw, b_dw, ln_g, ln_b):
    h = c_val * jax.nn.sigmoid(c_gate)
    h = lax.conv_general_dilated(
        h, w_dw[:, None, :], window_strides=(1,), padding=[(CONV_K - 1, 0)],
        dimension_numbers=('NWC', 'WIO', 'NWC'), feature_group_count=CONV_WIDTH) + b_dw
    return jax.nn.silu(layer_norm(h, ln_g, ln_b))


def hybrid_mixer(h, w_in, w_out, g_v, w_s, b_s, w_pool, s_pool, w_dw, b_dw, ln_g, ln_b):
    z = h @ w_in
    cuts = (GMLP_WIDTH, 2 * GMLP_WIDTH, 2 * GMLP_WIDTH + POOL_WIDTH,
            2 * GMLP_WIDTH + POOL_WIDTH + CONV_WIDTH)
    z_a, p_b, c_val, c_gate = (z[..., :cuts[1]], z[..., cuts[1]:cuts[2]],
                               z[..., cuts[2]:cuts[3]], z[..., cuts[3]:])
    z_a = jax.nn.gelu(z_a)
    y_a = spatial_gating(z_a[..., :GMLP_WIDTH], z_a[..., GMLP_WIDTH:], g_v, w_s, b_s)
    y_b = multiscale_pool(p_b, w_pool, s_pool)
    y_c = conformer_conv(c_val, c_gate, w_dw, b_dw, ln_g, ln_b)
    return jnp.concatenate([y_a, y_b, y_c], axis=-1) @ w_out


def cross_attention(h, m, w_q, w_k, w_v, w_o):
    B, S, _ = h.shape
    q = (h @ w_q).reshape(B, S, XATTN_HEADS, XATTN_HEAD_DIM)
    k = (m @ w_k).reshape(B, MEM_LEN, XATTN_HEADS, XATTN_HEAD_DIM)
    v = (m @ w_v).reshape(B, MEM_LEN, XATTN_HEADS, XATTN_HEAD_DIM)
    scores = jnp.einsum('bshd,bmhd->bhsm', q.astype(jnp.float32), k.astype(jnp.float32))
    probs = jax.nn.softmax(scores * (XATTN_HEAD_DIM ** -0.5), axis=-1).astype(v.dtype)
    out = jnp.einsum('bhsm,bmhd->bshd', probs, v).reshape(B, S, D_MODEL)
    return out @ w_o


def setup_inputs(seed: int = 0) -> dict:
    key = jax.random.key(seed)
    ks = iter(jax.random.split(key, 32))
    f32 = jnp.float32

    def nrm(shape, scale):
        return jax.random.normal(next(ks), shape, f32) * scale

    def gain(shape):
        return 1.0 + 0.05 * jax.random.normal(next(ks), shape, f32)

    L = DEPTH
    return {
        "x": jax.random.normal(next(ks), (BATCH, SEQ, D_MODEL), f32),
        "mem": jax.random.normal(next(ks), (BATCH, MEM_LEN, D_MODEL), f32),
        "norm_mix_pre": gain((L, D_MODEL)),
        "norm_mix_post": gain((L, D_MODEL)),
        "w_in": nrm((L, D_MODEL, IN_COLS), D_MODEL ** -0.5),
        "w_out": nrm((L, MIX_WIDTH, D_MODEL), MIX_WIDTH ** -0.5),
        "gmlp_v_gain": gain((L, GMLP_HEADS, HEAD_DIM)),
        "w_spatial": nrm((L, GMLP_HEADS, CHUNK, CHUNK), CHUNK ** -0.5),
        "b_spatial": gain((L, GMLP_HEADS, CHUNK)),
        "w_pool": nrm((L, POOL_GROUPS, POOL_GROUP_WIDTH, POOL_GROUP_WIDTH), POOL_GROUP_WIDTH ** -0.5),
        "s_pool": gain((L, POOL_WIDTH)),
        "w_dw": nrm((L, CONV_K, CONV_WIDTH), CONV_K ** -0.5),
        "b_dw": nrm((L, CONV_WIDTH), 0.02),
        "conv_ln_g": gain((L, CONV_WIDTH)),
        "conv_ln_b": nrm((L, CONV_WIDTH), 0.02),
        "norm_xattn_pre": gain((L, D_MODEL)),
        "norm_mem": gain((L, D_MODEL)),
        "norm_xattn_post": gain((L, D_MODEL)),
        "w_q": nrm((L, D_MODEL, D_MODEL), D_MODEL ** -0.5),
        "w_k": nrm((L, D_MODEL, D_MODEL), D_MODEL ** -0.5),
        "w_v": nrm((L, D_MODEL, D_MODEL), D_MODEL ** -0.5),
        "w_o": nrm((L, D_MODEL, D_MODEL), D_MODEL ** -0.5),
        "norm_ffn_pre": gain((L, D_MODEL)),
        "norm_ffn_post": gain((L, D_MODEL)),
        "w_up": nrm((L, D_MODEL, D_FF), D_MODEL ** -0.5),
        "w_down": nrm((L, D_FF, D_MODEL), D_FF ** -0.5),
    }


def reference(x, mem, norm_mix_pre, norm_mix_post, w_in, w_out, gmlp_v_gain, w_spatial,
              b_spatial, w_pool, s_pool, w_dw, b_dw, conv_ln_g, conv_ln_b, norm_xattn_pre,
              norm_mem, norm_xattn_post, w_q, w_k, w_v, w_o, norm_ffn_pre, norm_ffn_post,
              w_up, w_down):
    for l in range(DEPTH):
        h = rms_norm(x, norm_mix_pre[l])
        h = hybrid_mixer(h, w_in[l], w_out[l], gmlp_v_gain[l], w_spatial[l], b_spatial[l],
                         w_pool[l], s_pool[l], w_dw[l], b_dw[l], conv_ln_g[l], conv_ln_b[l])
        x = x + rms_norm(h, norm_mix_post[l])
        h = rms_norm(x, norm_xattn_pre[l])
        m = rms_norm(mem, norm_mem[l])
        h = cross_attention(h, m, w_q[l], w_k[l], w_v[l], w_o[l])
        x = x + rms_norm(h, norm_xattn_post[l])
        h = rms_norm(x, norm_ffn_pre[l])
        h = jnp.square(jax.nn.relu(h @ w_up[l])) @ w_down[l]
        x = x + rms_norm(h, norm_ffn_post[l])
    return x
```

```python
import contextlib
import numpy as np
import concourse.bass as bass
import concourse.mybir as mybir
from concourse.bass_utils import run_bass_kernel_spmd

F32 = mybir.dt.float32
BF16 = mybir.dt.bfloat16
AF = mybir.ActivationFunctionType
ALU = mybir.AluOpType

D = 2048
NT = 1152
HALO = 128
NREAL = 1024
L = 2
MEM = 256
DFF = 8192
KC = 16
GROUPS = [(0, 384), (384, 384), (768, 384)]
NTILE = NT // 128
VW = 272
V_MIXPRE, V_MIXPOST, V_XAPRE, V_MEM, V_XAPOST, V_FFNPRE, V_FFNPOST = 0, 16, 32, 48, 64, 80, 96
V_GV, V_SPOOL, V_BDW, V_LNG, V_LNB, V_WDW = 112, 120, 124, 128, 132, 136
RMS_EPS = 1e-6
LN_EPS = 1e-5
ESZ = {F32: 4, BF16: 2}
CELL = 256


class Op:
    __slots__ = ("eng", "fn", "deps", "signal", "sigval", "sem", "inc", "isdma")

    def __init__(self, eng, fn):
        self.eng = eng
        self.fn = fn
        self.deps = []
        self.signal = False
        self.sigval = 0
        self.sem = None
        self.inc = 1
        self.isdma = False


def _cells(ap):
    name = ap.name
    es = ESZ[ap.dtype]
    pat = ap.ap
    pstride = pat[0][0]
    off = ap.offset % pstride if pstride > 0 else ap.offset
    free = [(s, n) for (s, n) in pat[1:] if n > 1 and s != 0]
    if not free:
        free = [(1, 1)]
    free.sort(key=lambda x: x[0])
    s0, n0 = free[0]
    if s0 == 1:
        run = n0
        outer = free[1:]
    else:
        run = 1
        outer = free
    nout = 1
    for s, n in outer:
        nout *= n
    out = set()
    if nout <= 256:
        idxs = [0]
        for s, n in outer:
            idxs = [b + i * s for b in idxs for i in range(n)]
        for b in idxs:
            lo = (off + b) * es
            hi = (off + b + run) * es - 1
            for c in range(lo // CELL, hi // CELL + 1):
                out.add((name, c))
    else:
        span = run + sum((n - 1) * s for s, n in outer)
        lo = off * es
        hi = (off + span) * es - 1
        for c in range(lo // CELL, hi // CELL + 1):
            out.add((name, c))
    return out


class Sched:
    ENGS = ("pe", "act", "dve", "pool", "sp")

    def __init__(self, n_dma_sems=28):
        self.ops = {e: [] for e in self.ENGS}
        self.state = {}
        self.n_dma_sems = n_dma_sems
        self.dma_rr = 0
        self.dma_last = [None] * n_dma_sems
        self.dma_count = [0] * n_dma_sems

    def _dep(self, op, prod):
        if prod is op:
            return
        if prod.eng == "pe" and op.eng == "pe":
            return
        prod.signal = True
        op.deps.append(prod)

    def add(self, eng, fn, r=(), w=(), rk=(), wk=(), dma=False):
        op = Op(eng, fn)
        rkeys = set(rk)
        for ap in r:
            rkeys |= _cells(ap)
        wkeys = set(wk)
        for ap in w:
            wkeys |= _cells(ap)
        for k in rkeys:
            st = self.state.get(k)
            if st is None:
                st = self.state[k] = [[], []]
            for p in st[0]:
                self._dep(op, p)
        for k in wkeys:
            st = self.state.get(k)
            if st is None:
                st = self.state[k] = [[], []]
            for p in st[0]:
                self._dep(op, p)
            for p in st[1]:
                self._dep(op, p)
        for k in rkeys:
            if k not in wkeys:
                self.state[k][1].append(op)
        for k in wkeys:
            st = self.state[k]
            if st[1] or not st[0] or st[0][-1].eng != eng or eng not in ("pe", "pool", "sp"):
                st[0] = [op]
            else:
                st[0] = st[0][-3:] + [op]
            st[1] = []
        if dma:
            op.isdma = True
            op.signal = True
            i = self.dma_rr
            self.dma_rr = (i + 1) % self.n_dma_sems
            prev = self.dma_last[i]
            if prev is not None:
                op.deps.append(prev)
            self.dma_last[i] = op
            self.dma_count[i] += 16
            op.sem = ("dma", i)
            op.inc = 16
            op.sigval = self.dma_count[i]
        self.ops[eng].append(op)
        return op

    def emit(self, nc):
        for e in self.ENGS:
            cnt = 0
            for op in self.ops[e]:
                if op.isdma:
                    continue
                op.sem = ("eng", e)
                if op.signal:
                    cnt += 1
                    op.sigval = cnt
        with contextlib.ExitStack() as es:
            sems = {}
            for e in self.ENGS:
                sems[("eng", e)] = es.enter_context(nc.semaphore("s_" + e))
            for i in range(self.n_dma_sems):
                sems[("dma", i)] = es.enter_context(nc.semaphore("d_%d" % i))
            block = es.enter_context(nc.Block())

            def run(ename, eng):
                waited = {}
                for op in self.ops[ename]:
                    need = {}
                    for p in op.deps:
                        if need.get(p.sem, 0) < p.sigval:
                            need[p.sem] = p.sigval
                    for sk, v in need.items():
                        if waited.get(sk, 0) < v:
                            eng.wait_ge(sems[sk], v)
                            waited[sk] = v
                    inst = op.fn(eng)
                    if op.signal:
                        inst.then_inc(sems[op.sem], op.inc)

            @block.tensor
            def _(pe):
                run("pe", pe)

            @block.scalar
            def _(a):
                run("act", a)

            @block.vector
            def _(v):
                run("dve", v)

            @block.gpsimd
            def _(g):
                run("pool", g)

            @block.sync
            def _(s):
                run("sp", s)


def build_nc(n_layers=L, dbg=None):
    nc = bass.Bass("TRN2", target_bir_lowering=False)
    S = Sched()

    def din(name, shape):
        return nc.dram_tensor(name, list(shape), F32, kind="ExternalInput").ap()

    xT = din("xT", [D, NT])
    memT = din("memT", [D, MEM])
    vecs = din("vecs", [128, n_layers * VW])
    bspd = din("bsp", [n_layers, 128, 1024])
    wsTd = din("wsT", [n_layers, 128, 1024])
    wpd = din("wpl", [n_layers, 128, 512])
    cmask = din("cmask", [128, 1])
    ctab = din("ctab", [128, 64])
    w_in = [din("w_in_%d" % li, [28, 128, 2048]) for li in range(n_layers)]
    w_out = [din("w_out_%d" % li, [16, 128, 2048]) for li in range(n_layers)]
    w_q = [din("w_q_%d" % li, [16, 128, 2048]) for li in range(n_layers)]
    w_k = [din("w_k_%d" % li, [16, 128, 2048]) for li in range(n_layers)]
    w_v = [din("w_v_%d" % li, [16, 128, 2048]) for li in range(n_layers)]
    w_o = [din("w_o_%d" % li, [16, 128, 2048]) for li in range(n_layers)]
    w_up = [din("w_up_%d" % li, [64, 128, 2048]) for li in range(n_layers)]
    w_dn = [din("w_dn_%d" % li, [64, 128, 2048]) for li in range(n_layers)]
    yT = nc.dram_tensor("yT", [D, NREAL], F32, kind="ExternalOutput").ap()
    xs = nc.dram_tensor("xs", [D, NT], F32, kind="Internal").ap()

    es = contextlib.ExitStack()
    with es:
        SB = es.enter_context(nc.sbuf_tensor("SB", [128, 45056], F32))
        PS = es.enter_context(nc.psum_tensor("PS", [128, 8, 512], F32))

        def fv(lo, n):
            return SB[:, lo:lo + n]

        def bv(lo, nb):
            return SB[:, lo:lo + nb // 2].bitcast(BF16)

        A0, B10, B20, WR0, T0, C0 = 0, 18432, 27648, 36864, 40960, 43264
        A = fv(A0, 16 * NT).rearrange("p (c t) -> p c t", c=16)
        B1 = bv(B10, 16 * NT).rearrange("p (c t) -> p c t", c=16)
        B2 = bv(B20, 16 * NT).rearrange("p (c t) -> p c t", c=16)
        WR = [bv(WR0 + 1024 * i, 2048) for i in range(4)]
        R = fv(T0, NT)
        SQ = [bv(T0 + 1152, NT), bv(T0 + 1152 + 576, NT)]
        SQF = [fv(T0 + 1152, 576), fv(T0 + 1152 + 576, 576)]
        c = C0
        VEC = fv(c, L * VW); c += L * VW
        WMT = bv(c, 1024).rearrange("p (h t) -> p h t", h=8); c += 512
        WPL = bv(c, 512).rearrange("p (g d) -> p g d", g=4); c += 256
        ONES = bv(c, 128); c += 64
        ONESL = bv(c, 128); c += 64
        MASK = fv(c, 1); c += 4
        TAB = fv(c, 64).rearrange("p (g t) -> p g t", g=4); c += 64
        assert c <= 45056, c
        XSG = [fv(B20, NT), fv(B20 + NT, NT)]
        TMPF = [fv(B20 + 2 * NT, NT), fv(B20 + 3 * NT, NT)]

        STB = [PS[:, i, :] for i in range(3)]
        MMB = [PS[:, 3 + i, :] for i in range(5)]
        mm_rr = [0]

        def mmbank():
            b = MMB[mm_rr[0]]
            mm_rr[0] = (mm_rr[0] + 1) % 5
            return b

        def matmul(out, lhsT, rhs, start, stop):
            S.add("pe", lambda e: e.matmul(out, lhsT=lhsT, rhs=rhs, start=start, stop=stop),
                  r=[lhsT, rhs], w=[out])

        def act(out, in_, func, bias=None, scale=None, eng="act"):
            kw = {}
            r = [in_]
            if bias is not None:
                kw["bias"] = bias
                if not isinstance(bias, (int, float)):
                    r.append(bias)
            if scale is not None:
                kw["scale"] = scale
                if not isinstance(scale, (int, float)):
                    r.append(scale)
            S.add("act", lambda e: e.activation(out=out, in_=in_, func=func, **kw), r=r, w=[out])

        def tt(out, in0, in1, op, eng="dve"):
            S.add(eng, lambda e: e.tensor_tensor(out=out, in0=in0, in1=in1, op=op), r=[in0, in1], w=[out])

        def ts(out, in0, s1, s2, op0, op1=None, eng="dve"):
            r = [in0]
            if not isinstance(s1, (int, float)):
                r.append(s1)
            if s2 is not None and not isinstance(s2, (int, float)):
                r.append(s2)
            if op1 is None:
                S.add(eng, lambda e: e.tensor_scalar(out=out, in0=in0, scalar1=s1, scalar2=None, op0=op0), r=r, w=[out])
            else:
                S.add(eng, lambda e: e.tensor_scalar(out=out, in0=in0, scalar1=s1, scalar2=s2, op0=op0, op1=op1),
                      r=r, w=[out])

        def stt(out, in0, sc, in1, op0, op1):
            r = [in0, in1]
            if not isinstance(sc, (int, float)):
                r.append(sc)
            S.add("dve", lambda e: e.scalar_tensor_tensor(out=out, in0=in0, scalar=sc, in1=in1, op0=op0, op1=op1),
                  r=r, w=[out])

        def dma(q, out, in_, r=(), w=(), rk=(), wk=()):
            return S.add(q, lambda e: e.dma_start(out=out, in_=in_), r=r, w=w, rk=rk, wk=wk, dma=True)

        def memset(ap, val, eng="dve"):
            S.add(eng, lambda e: e.memset(ap, val), w=[ap])

        wr_rr = [0]

        def load_w(src):
            i = wr_rr[0]
            wr_rr[0] = (i + 1) % 4
            slot = WR[i]
            dma("pool", slot, src, w=[slot])
            return slot.rearrange("p (k n) -> p k n", k=KC)

        pending = []

        def flush_pending():
            while pending:
                pending.pop(0)()

        def proj(wsrc, inb, evac, groups=GROUPS):
            wt = load_w(wsrc)
            for gi, (t0, n) in enumerate(groups):
                ps = mmbank()[:, :n]
                for kc in range(KC):
                    matmul(ps, wt[:, kc, :], inb[:, kc, t0:t0 + n], kc == 0, kc == KC - 1)
                if gi == 1:
                    flush_pending()
                evac(gi, t0, n, ps)

        def rstd_from(ps, out, eps):
            act(out, ps, AF.Ln, bias=EPSC[eps], scale=1.0 / D)
            act(out, out, AF.Exp, scale=-0.5)

        IDENT = bv(c, 128); c += 64
        EPS_R = fv(c, 1); c += 4
        EPS_L = fv(c, 1); c += 4
        assert c <= 45056
        EPSC = {RMS_EPS: EPS_R, LN_EPS: EPS_L}
        memset(EPS_R, RMS_EPS)
        memset(EPS_L, LN_EPS)
        memset(ONES, 1.0)
        memset(ONESL, 1.0 / 512.0)
        memset(IDENT, 1.0)
        S.add("pool", lambda e: e.affine_select(out=IDENT, in_=IDENT, pattern=[[1, 128]], compare_op=ALU.is_equal,
                                                fill=0.0, base=0, channel_multiplier=-1), r=[IDENT], w=[IDENT])
        dma("sp", VEC[:, 0:n_layers * VW], vecs, w=[VEC])
        dma("sp", MASK, cmask, w=[MASK])
        dma("sp", TAB.rearrange("p g t -> p (g t)"), ctab, w=[TAB])

        def vcol(l, base, i):
            return VEC[:, l * VW + base + i: l * VW + base + i + 1]

        slot_rr = [0]

        def sqslot(n):
            k = slot_rr[0]
            slot_rr[0] += 1
            if n <= 384:
                return SQ[k % 2][:, (k // 2 % 3) * 384:(k // 2 % 3) * 384 + n]
            return SQ[k % 2][:, (k // 2 % 2) * 512:(k // 2 % 2) * 512 + n]

        def prenorm(l, vbase, first, groups):
            lo = groups[0][0]
            if first:
                for cch in range(KC):
                    dma("sp", A[:, cch, :], xT[cch * 128:(cch + 1) * 128, :], w=[A[:, cch, :]])
            for cch in range(KC):
                sq = SQ[cch % 2]
                act(sq[:, lo:NT], A[:, cch, lo:NT], AF.Square)
                for gi, (t0, n) in enumerate(groups):
                    matmul(STB[gi][:, :n], ONES, sq[:, t0:t0 + n], cch == 0, cch == KC - 1)
            for gi, (t0, n) in enumerate(groups):
                rstd_from(STB[gi][:, :n], MMB[gi][:, :n], RMS_EPS)
            for cch in range(KC):
                for gi, (t0, n) in enumerate(groups):
                    stt(B1[:, cch, t0:t0 + n], A[:, cch, t0:t0 + n], vcol(l, vbase, cch), MMB[gi][:, :n],
                        ALU.mult, ALU.mult)

        RING = [fv(B10 + i * NT, NT) for i in range(8)]

        def postadd(l, vbase, src, xsrc_is_input, to_output, groups, inplace):
            lo = groups[0][0]
            for gi, (t0, n) in enumerate(groups):
                rstd_from(STB[gi][:, :n], MMB[gi][:, :n], RMS_EPS)
            def xbuf(j):
                return A[:, j, :] if inplace else RING[j % 8]

            def xload(j):
                rows = slice(j * 128, (j + 1) * 128)
                xg = xbuf(j)[:, lo:NT]
                if xsrc_is_input:
                    dma("sp", xg, xT[rows, lo:NT], w=[xg])
                else:
                    dma("sp", xg, xs[rows, lo:NT], w=[xg], rk=[("xs", j)])

            npre = KC if inplace else 8
            for j in range(npre):
                xload(j)
            k = 0
            for j in range(KC):
                rows = slice(j * 128, (j + 1) * 128)
                xfull = xbuf(j)
                for gi, (t0, n) in enumerate(groups):
                    tp = MMB[3 + k % 2][:, :n]
                    k += 1
                    stt(tp, src[:, j, t0:t0 + n], vcol(l, vbase, j), MMB[gi][:, :n], ALU.mult, ALU.mult)
                    tt(A[:, j, t0:t0 + n], tp, xfull[:, t0:t0 + n], ALU.add)
                if to_output:
                    outd.append(dma("sp", yT[rows, :], A[:, j, HALO:NT], r=[A[:, j, HALO:NT]]))
                else:
                    dma("sp", xs[rows, lo:NT], A[:, j, lo:NT], r=[A[:, j, lo:NT]], wk=[("xs", j)])
                if j + npre < KC:
                    xload(j + npre)

        def evac_h2(j):
            def ev(gi, t0, n, ps):
                act(B1[:, j, t0:t0 + n], ps, AF.Copy)
                slot = sqslot(n)
                act(slot, ps, AF.Square)
                pending.append(lambda: matmul(STB[gi][:, :n], ONES, slot, j == 0, j == KC - 1))
            return ev

        outd = []

        GA = [(96, 352), (448, 352), (800, 352)]
        GB = [(128, 512), (640, 512)]
        for l in range(n_layers):
            last = (l == n_layers - 1)
            GM = GROUPS if l == 0 else GA
            GP = GB if last else GA
            GU = GROUPS if l == 0 else GB
            tblocks = [(0, 4), (4, 4), (8, 1)] if l == 0 else [(1, 4), (5, 4)]
            tl0 = tblocks[0][0]
            prenorm(l, V_MIXPRE, first=(l == 0), groups=GM)
            BSP = fv(A0 + 17408, 1024).rearrange("p (h t) -> p h t", h=8)
            dma("sp", BSP.rearrange("p h t -> p (h t)"), bspd[l], w=[BSP])
            dma("pool", WMT.rearrange("p h t -> p (h t)"), wsTd[l], w=[WMT])
            S.add("pool", lambda e: e.affine_select(out=WMT, in_=WMT, pattern=[[0, 8], [1, 128]],
                                                    compare_op=ALU.is_ge, fill=0.0, base=0, channel_multiplier=-1),
                  r=[WMT], w=[WMT])
            dma("pool", WPL.rearrange("p g d -> p (g d)"), wpd[l], w=[WPL])

            P = fv(A0 + 12416, 4 * 1168).rearrange("p (g t) -> p g t", g=4)
            PL = bv(A0 + 10512, NT)
            S1 = fv(A0 + 11088, 1168)
            S2 = fv(A0 + 9344, 1168)
            memset(P[:, :, 0:16], 0.0)
            for g in range(4):
                def ev(gi, t0, n, ps, g=g):
                    act(P[:, g, 16 + t0:16 + t0 + n], ps, AF.Copy)
                proj(w_in[l][16 + g], B1, ev, groups=GM)
            def pool_group(g):
                if True:
                    w = 2 << g
                    cur = P[:, g, :]
                    bufs = [S1, S2]
                    for s in range(g + 1):
                        sh = 1 << s
                        nxt = bufs[s % 2]
                        tt(nxt[:, sh:1168], cur[:, sh:1168], cur[:, 0:1168 - sh], ALU.add)
                        cur = nxt
                    stt(PL, cur[:, 16:1168], 1.0 / w, P[:, g, 16:1168], ALU.mult, ALU.subtract)
                    t1 = SQF[0][:, 0:16]
                    tt(t1, cur[:, 16 + HALO:16 + HALO + 16], TAB[:, g, :], ALU.mult)
                    tt(PL[:, HALO:HALO + 16], t1, P[:, g, 16 + HALO:16 + HALO + 16], ALU.subtract)
                    for gi, (t0, n) in enumerate(GM):
                        ps = mmbank()[:, :n]
                        matmul(ps, WPL[:, g, :], PL[:, t0:t0 + n], True, True)
                        ts(B2[:, 8 + g, t0:t0 + n], ps, vcol(l, V_SPOOL, g), None, ALU.mult)

            H = bv(A0, 4 * 1184).rearrange("p (g t) -> p g t", g=4)
            DG = bv(A0 + 2368, 31 * 128).rearrange("p (j c) -> p j c", j=31)
            CO = fv(A0 + 4736, 4 * NT).rearrange("p (g t) -> p g t", g=4)
            SIG = fv(A0 + 9344, NT)
            LT = [fv(A0 + 10496 + 384 * i, 384) for i in range(5)]
            CS4 = [bv(A0 + 9344 + 192 * i, 384) for i in range(4)]
            memset(H[:, :, 0:32], 0.0)
            for cc in range(4):
                def evg(gi, t0, n, ps):
                    act(SIG[:, t0:t0 + n], ps, AF.Sigmoid)
                proj(w_in[l][24 + cc], B1, evg, groups=GM)

                def evv(gi, t0, n, ps, cc=cc):
                    tt(H[:, cc, 32 + t0:32 + t0 + n], ps, SIG[:, t0:t0 + n], ALU.mult)
                proj(w_in[l][20 + cc], B1, evv, groups=GM)
            ts(H[:, :, 32:32 + HALO], H[:, :, 32:32 + HALO], MASK, None, ALU.mult)
            ts(P[:, :, 16:16 + HALO], P[:, :, 16:16 + HALO], MASK, None, ALU.mult)

            def conv_diag(cc):
                for j in range(31):
                    ts(DG[:, j, :], IDENT, vcol(l, V_WDW, cc * 31 + j), None, ALU.mult)
                for gi, (t0, n) in enumerate(GM):
                    ps = mmbank()[:, :n]
                    for j in range(31):
                        matmul(ps, DG[:, j, :], H[:, cc, 2 + t0 + j:2 + t0 + j + n], j == 0, j == 30)
                    act(CO[:, cc, t0:t0 + n], ps, AF.Identity, bias=vcol(l, V_BDW, cc))

            for i in range(4):
                conv_diag(i)
                pool_group(i)

            def conv_ln(gis):
              for gi in gis:
                t0, n = GM[gi]
                psM = mmbank()[:, :n]
                psQ = mmbank()[:, :n]
                for cc in range(4):
                    cb = SQ[0][:, cc % 3 * 384:cc % 3 * 384 + n] if cc < 3 else SQ[1][:, 0:n]
                    cs = CS4[cc][:, :n]
                    act(cb, CO[:, cc, t0:t0 + n], AF.Copy)
                    act(cs, CO[:, cc, t0:t0 + n], AF.Square)
                    matmul(psM, ONESL, cb, cc == 0, cc == 3)
                    matmul(psQ, ONESL, cs, cc == 0, cc == 3)
                mean, msq, var, t1a, t1b = [x[:, :n] for x in LT]
                act(mean, psM, AF.Copy)
                act(msq, psM, AF.Square)
                tt(var, psQ, msq, ALU.subtract)
                act(var, var, AF.Ln, bias=EPS_L, scale=1.0)
                act(var, var, AF.Exp, scale=-0.5)
                for cc in range(4):
                    t1 = (t1a, t1b)[cc % 2]
                    tt(t1, CO[:, cc, t0:t0 + n], mean, ALU.subtract)
                    tt(t1, t1, var, ALU.mult)
                    act(B2[:, 12 + cc, t0:t0 + n], t1, AF.Silu, bias=vcol(l, V_LNB, cc), scale=vcol(l, V_LNG, cc))

            G0 = A0 + 12416
            UT = [bv(G0, NT), bv(G0 + 576, NT)]
            VG = [fv(G0 + 1152, NT), fv(G0 + 2304, NT)]
            VN = [bv(G0 + 3456, NT), bv(G0 + 3456 + 576, NT)]
            BS = fv(G0 + 4608, 54).rearrange("p (t s) -> p t s", t=9)
            MV = fv(G0 + 4608 + 64, 18).rearrange("p (t s) -> p t s", t=9)
            RSD = fv(G0 + 4608 + 96, 9)
            def gmlp_proj(h):
                ut = UT[h % 2]
                vg = VG[h % 2]

                def evu(gi, t0, n, ps, ut=ut):
                    act(ut[:, t0:t0 + n], ps, AF.Gelu_apprx_tanh)
                proj(w_in[l][h], B1, evu, groups=GU)
                wt = load_w(w_in[l][8 + h])
                for (tb0, ntb) in tblocks:
                    ps = mmbank()
                    for i in range(ntb):
                        ti = tb0 + i
                        for kc in range(KC):
                            matmul(ps[:, i * 128:(i + 1) * 128], B1[:, kc, ti * 128:(ti + 1) * 128], wt[:, kc, :],
                                   kc == 0, kc == KC - 1)
                    act(vg[:, tb0 * 128:(tb0 + ntb) * 128], ps[:, :ntb * 128], AF.Gelu_apprx_tanh)

            def gmlp_rest(h):
                ut = UT[h % 2]
                vg = VG[h % 2]
                vn = VN[h % 2]
                for ti in range(tl0, NTILE):
                    S.add("dve", lambda e, ti=ti, vg=vg: e.bn_stats(out=BS[:, ti, :], in_=vg[:, ti * 128:(ti + 1) * 128]),
                          r=[vg[:, ti * 128:(ti + 1) * 128]], w=[BS[:, ti, :]])
                    S.add("dve", lambda e, ti=ti: e.bn_aggr(out=MV[:, ti, :], in_=BS[:, ti, :]),
                          r=[BS[:, ti, :]], w=[MV[:, ti, :]])
                act(RSD[:, tl0:NTILE], MV[:, tl0:NTILE, 1], AF.Ln, bias=EPS_L, scale=1.0)
                act(RSD[:, tl0:NTILE], RSD[:, tl0:NTILE], AF.Exp, scale=-0.5)
                for ti in range(tl0, NTILE):
                    ts(vn[:, ti * 128:(ti + 1) * 128], vg[:, ti * 128:(ti + 1) * 128], MV[:, ti, 0:1], RSD[:, ti:ti + 1],
                       ALU.subtract, ALU.mult)
                for bi, (tb0, ntb) in enumerate(tblocks):
                    ps = mmbank()
                    for i in range(ntb):
                        ti = tb0 + i
                        matmul(ps[:, i * 128:(i + 1) * 128], vn[:, ti * 128:(ti + 1) * 128], WMT[:, h, :], True, True)
                    t1 = SQF[(h * 3 + bi) % 2][:, 0:ntb * 128]
                    for i in range(ntb):
                        stt(t1[:, i * 128:(i + 1) * 128], ps[:, i * 128:(i + 1) * 128], vcol(l, V_GV, h), BSP[:, h, :],
                            ALU.mult, ALU.add)
                    tt(B2[:, h, tb0 * 128:(tb0 + ntb) * 128], t1, ut[:, tb0 * 128:(tb0 + ntb) * 128], ALU.mult)

            gmlp_proj(0)
            conv_ln([0])
            gmlp_proj(1)
            conv_ln([1, 2])
            for h in range(8):
                if 2 <= h + 1 < 8:
                    gmlp_proj(h + 1)
                gmlp_rest(h)

            for j in range(KC):
                proj(w_out[l][j], B2, evac_h2(j), groups=GP)
            flush_pending()
            postadd(l, V_MIXPOST, B1, xsrc_is_input=(l == 0), to_output=False, groups=GP, inplace=True)
            if dbg == "mix" and l == n_layers - 1:
                break

            prenorm(l, V_XAPRE, first=False, groups=GP)
            MT = fv(A0, 16 * MEM).rearrange("p (c m) -> p c m", c=16)
            MN = bv(A0 + 4096, 16 * MEM).rearrange("p (c m) -> p c m", c=16)
            KT = bv(A0 + 6144, 16 * MEM).rearrange("p (c m) -> p c m", c=16)
            VV = bv(A0 + 8192, 2 * D).rearrange("p (m d) -> p m d", m=2)
            QH = [bv(A0 + 10240, 4 * NT).rearrange("p (c t) -> p c t", c=4),
                  bv(A0 + 12544, 4 * NT).rearrange("p (c t) -> p c t", c=4)]
            EE = [bv(A0 + 14848, 1024).rearrange("p (m t) -> p m t", m=2),
                  bv(A0 + 15360, 1024).rearrange("p (m t) -> p m t", m=2)]
            RINV = [fv(A0 + 15872, 512), fv(A0 + 16384, 512)]
            dma("sp", MT, memT.rearrange("(c p) m -> p c m", p=128), w=[MT])
            psA = mmbank()[:, :MEM]
            for cch in range(KC):
                sq = SQ[cch % 2][:, 0:MEM]
                act(sq, MT[:, cch, :], AF.Square)
                matmul(psA, ONES, sq, cch == 0, cch == KC - 1)
            RM = R[:, 0:MEM]
            rstd_from(psA, RM, RMS_EPS)
            for cch in range(KC):
                stt(MN[:, cch, :], MT[:, cch, :], vcol(l, V_MEM, cch), RM, ALU.mult, ALU.mult)
            for j in range(KC):
                def evk(gi, t0, n, ps, j=j):
                    act(KT[:, j, :], ps, AF.Copy)
                proj(w_k[l][j], MN, evk, groups=[(0, MEM)])
            for j in range(KC):
                wt = load_w(w_v[l][j])
                ps = mmbank()
                for mt in range(2):
                    for kc in range(KC):
                        matmul(ps[:, mt * 128:(mt + 1) * 128], MN[:, kc, mt * 128:(mt + 1) * 128], wt[:, kc, :],
                               kc == 0, kc == KC - 1)
                act(VV[:, :, j * 128:(j + 1) * 128], ps[:, 0:256].rearrange("p (m d) -> p m d", m=2), AF.Copy)
            scale = 512.0 ** -0.5
            it = 0

            def q_proj(h):
                qh = QH[h % 2]
                for dc in range(4):
                    def evq(gi, t0, n, ps, dc=dc, qh=qh):
                        act(qh[:, dc, t0:t0 + n], ps, AF.Copy)
                    proj(w_q[l][h * 4 + dc], B1, evq, groups=GP)

            def s_part(h, gi, itn):
                t0, n = GP[gi]
                qh = QH[h % 2]
                ee = EE[itn % 2]
                for mt in range(2):
                    psS = mmbank()[:, :n]
                    for dc in range(4):
                        matmul(psS, KT[:, h * 4 + dc, mt * 128:(mt + 1) * 128], qh[:, dc, t0:t0 + n], dc == 0, dc == 3)
                    act(ee[:, mt, :n], psS, AF.Exp, scale=scale)

            def pv_part(h, gi, itn):
                t0, n = GP[gi]
                ee = EE[itn % 2]
                rinv = RINV[itn % 2][:, :n]
                psD = mmbank()[:, :n]
                for mt in range(2):
                    matmul(psD, ONES, ee[:, mt, :n], mt == 0, mt == 1)
                S.add("dve", lambda e, rinv=rinv, psD=psD: e.reciprocal(out=rinv, in_=psD), r=[psD], w=[rinv])
                for dc in range(4):
                    psO = mmbank()[:, :n]
                    for mt in range(2):
                        matmul(psO, VV[:, mt, h * 512 + dc * 128:h * 512 + (dc + 1) * 128], ee[:, mt, :n], mt == 0, mt == 1)
                    tt(B2[:, h * 4 + dc, t0:t0 + n], psO, rinv, ALU.mult)

            q_proj(0)
            for h in range(4):
                if h + 1 < 4:
                    q_proj(h + 1)
                s_part(h, 0, it)
                for gi in range(len(GP)):
                    if gi + 1 < len(GP):
                        s_part(h, gi + 1, it + 1)
                    pv_part(h, gi, it)
                    it += 1
            for j in range(KC):
                proj(w_o[l][j], B2, evac_h2(j), groups=GP)
            flush_pending()
            postadd(l, V_XAPOST, B1, xsrc_is_input=False, to_output=False, groups=GP, inplace=True)
            if dbg == "xa" and l == n_layers - 1:
                break

            prenorm(l, V_FFNPRE, first=False, groups=GP)
            for hb in range(4):
                for hc in range(KC):
                    def evr(gi, t0, n, ps, hc=hc):
                        rt = sqslot(n)
                        act(rt, ps, AF.Relu)
                        stt(B2[:, hc, t0:t0 + n], ps, 0.0, rt, ALU.max, ALU.mult)
                    proj(w_up[l][hb * 16 + hc], B1, evr, groups=GP)
                for j in range(KC):
                    def evd(gi, t0, n, ps, j=j, hb=hb):
                        if hb == 0:
                            act(A[:, j, t0:t0 + n], ps, AF.Copy)
                        else:
                            tt(A[:, j, t0:t0 + n], ps, A[:, j, t0:t0 + n], ALU.add)
                    proj(w_dn[l][hb * 16 + j], B2, evd, groups=GP)
            lo_p = GP[0][0]
            for j in range(KC):
                sq = SQ[j % 2]
                act(sq[:, lo_p:NT], A[:, j, lo_p:NT], AF.Square)
                for gi, (t0, n) in enumerate(GP):
                    matmul(STB[gi][:, :n], ONES, sq[:, t0:t0 + n], j == 0, j == KC - 1)
            postadd(l, V_FFNPOST, A, xsrc_is_input=False, to_output=last, groups=GP, inplace=False)

        if not outd:
            for j in range(KC):
                rows = slice(j * 128, (j + 1) * 128)
                outd.append(dma("sp", yT[rows, :], A[:, j, HALO:NT], r=[A[:, j, HALO:NT]]))
        fin = S.add("sp", lambda e: e.nop(), r=(), w=())
        for o in outd:
            fin.deps.append(o)
        S.emit(nc)
    return nc


def _tile_w(w):
    K, C = w.shape
    nb = K // 2048
    t = w.reshape(nb, 16, 128, C // 128, 128).transpose(0, 3, 2, 1, 4)
    return np.ascontiguousarray(t).reshape(nb * (C // 128), 128, 2048)


def _prep_shared(inp, layers=(0, 1)):
    f = np.float32
    sh = {}
    for name, key in (("w_in", "w_in"), ("w_out", "w_out"), ("w_q", "w_q"), ("w_k", "w_k"), ("w_v", "w_v"),
                      ("w_o", "w_o"), ("w_up", "w_up"), ("w_dn", "w_down")):
        w = np.asarray(inp[key], dtype=f)
        for li, l in enumerate(layers):
            sh["%s_%d" % (name, li)] = _tile_w(w[l])
    NL = len(layers)
    vec = np.zeros((128, NL * VW), f)
    for li, l in enumerate(layers):
        b = li * VW
        for base, key in ((V_MIXPRE, "norm_mix_pre"), (V_MIXPOST, "norm_mix_post"), (V_XAPRE, "norm_xattn_pre"),
                          (V_MEM, "norm_mem"), (V_XAPOST, "norm_xattn_post"), (V_FFNPRE, "norm_ffn_pre"),
                          (V_FFNPOST, "norm_ffn_post")):
            vec[:, b + base:b + base + 16] = np.asarray(inp[key], f)[l].reshape(16, 128).T
        vec[:, b + V_GV:b + V_GV + 8] = np.asarray(inp["gmlp_v_gain"], f)[l].T
        for base, key in ((V_SPOOL, "s_pool"), (V_BDW, "b_dw"), (V_LNG, "conv_ln_g"), (V_LNB, "conv_ln_b")):
            vec[:, b + base:b + base + 4] = np.asarray(inp[key], f)[l].reshape(4, 128).T
        vec[:, b + V_WDW:b + V_WDW + 124] = np.asarray(inp["w_dw"], f)[l].reshape(31, 4, 128).transpose(2, 1, 0).reshape(128, 124)
    sh["vecs"] = vec
    lay = list(layers)
    bs = np.asarray(inp["b_spatial"], f)[lay]
    sh["bsp"] = np.ascontiguousarray(np.broadcast_to(bs.reshape(NL, 1, 1024), (NL, 128, 1024)))
    ws = np.asarray(inp["w_spatial"], f)[lay]
    sh["wsT"] = np.ascontiguousarray(ws.transpose(0, 3, 1, 2)).reshape(NL, 128, 1024)
    wp = np.asarray(inp["w_pool"], f)[lay]
    sh["wpl"] = np.ascontiguousarray(wp.transpose(0, 2, 1, 3)).reshape(NL, 128, 512)
    return sh


def _prep_core(inp, core, x=None):
    f = np.float32
    x = np.asarray(inp["x"], f) if x is None else x
    mem = np.asarray(inp["mem"], f)
    b, half = core // 2, core % 2
    s0 = half * NREAL
    xt = np.zeros((D, NT), f)
    if half == 1:
        xt[:, :] = x[b, s0 - HALO:s0 + NREAL, :].T
    else:
        xt[:, HALO:] = x[b, 0:NREAL, :].T
    d = {"xT": xt, "memT": np.ascontiguousarray(mem[b].T)}
    d["cmask"] = np.full((128, 1), 1.0 if half == 1 else 0.0, f)
    tab = np.zeros((4, 16), f)
    for g, w in enumerate((2, 4, 8, 16)):
        for t in range(16):
            cnt = float(w) if half == 1 else float(min(t + 1, w))
            tab[g, t] = 1.0 / cnt
    d["ctab"] = np.ascontiguousarray(np.broadcast_to(tab.reshape(1, 64), (128, 64)))
    return d


_NC_CACHE = {}

FUSED = True


def _gather(res):
    out = np.zeros((4, 2048, D), np.float32)
    for core in range(8):
        b, half = core // 2, core % 2
        out[b, half * NREAL:(half + 1) * NREAL, :] = res.results[core]["yT"].T
    return out


def kernel(**inputs):
    if FUSED:
        sh = _prep_shared(inputs, (0, 1))
        in_maps = []
        for core in range(8):
            d = dict(sh)
            d.update(_prep_core(inputs, core))
            in_maps.append(d)
        if "nc2" not in _NC_CACHE:
            _NC_CACHE["nc2"] = build_nc(2)
        res = run_bass_kernel_spmd(_NC_CACHE["nc2"], in_maps, core_ids=list(range(8)))
        return _gather(res)
    if "nc1" not in _NC_CACHE:
        _NC_CACHE["nc1"] = build_nc(1)
    x = np.asarray(inputs["x"], np.float32)
    for l in range(L):
        sh = _prep_shared(inputs, (l,))
        in_maps = []
        for core in range(8):
            d = dict(sh)
            d.update(_prep_core(inputs, core, x=x))
            in_maps.append(d)
        res = run_bass_kernel_spmd(_NC_CACHE["nc1"], in_maps, core_ids=list(range(8)))
        x = _gather(res)
    return x
```

```python
import contextlib
import numpy as np
import concourse.bass as bass
import concourse.mybir as mybir
from concourse.bass_utils import run_bass_kernel_spmd

F32 = mybir.dt.float32
BF16 = mybir.dt.bfloat16
AF = mybir.ActivationFunctionType
ALU = mybir.AluOpType

D = 2048
NT = 1152
HALO = 128
NREAL = 1024
L = 2
MEM = 256
DFF = 8192
KC = 16
GROUPS = [(0, 384), (384, 384), (768, 384)]
NTILE = NT // 128
VW = 272
V_MIXPRE, V_MIXPOST, V_XAPRE, V_MEM, V_XAPOST, V_FFNPRE, V_FFNPOST = 0, 16, 32, 48, 64, 80, 96
V_GV, V_SPOOL, V_BDW, V_LNG, V_LNB, V_WDW = 112, 120, 124, 128, 132, 136
RMS_EPS = 1e-6
LN_EPS = 1e-5
ESZ = {F32: 4, BF16: 2}
CELL = 256


class Op:
    __slots__ = ("eng", "fn", "deps", "signal", "sigval", "sem", "inc", "isdma")

    def __init__(self, eng, fn):
        self.eng = eng
        self.fn = fn
        self.deps = []
        self.signal = False
        self.sigval = 0
        self.sem = None
        self.inc = 1
        self.isdma = False


def _cells(ap):
    name = ap.name
    es = ESZ[ap.dtype]
    pat = ap.ap
    pstride = pat[0][0]
    off = ap.offset % pstride if pstride > 0 else ap.offset
    free = [(s, n) for (s, n) in pat[1:] if n > 1 and s != 0]
    if not free:
        free = [(1, 1)]
    free.sort(key=lambda x: x[0])
    s0, n0 = free[0]
    if s0 == 1:
        run = n0
        outer = free[1:]
    else:
        run = 1
        outer = free
    nout = 1
    for s, n in outer:
        nout *= n
    out = set()
    if nout <= 256:
        idxs = [0]
        for s, n in outer:
            idxs = [b + i * s for b in idxs for i in range(n)]
        for b in idxs:
            lo = (off + b) * es
            hi = (off + b + run) * es - 1
            for c in range(lo // CELL, hi // CELL + 1):
                out.add((name, c))
    else:
        span = run + sum((n - 1) * s for s, n in outer)
        lo = off * es
        hi = (off + span) * es - 1
        for c in range(lo // CELL, hi // CELL + 1):
            out.add((name, c))
    return out


class Sched:
    ENGS = ("pe", "act", "dve", "pool", "sp")

    def __init__(self, n_dma_sems=28):
        self.ops = {e: [] for e in self.ENGS}
        self.state = {}
        self.n_dma_sems = n_dma_sems
        self.dma_rr = 0
        self.dma_last = [None] * n_dma_sems
        self.dma_count = [0] * n_dma_sems

    def _dep(self, op, prod):
        if prod is op:
            return
        if prod.eng == "pe" and op.eng == "pe":
            return
        prod.signal = True
        op.deps.append(prod)

    def add(self, eng, fn, r=(), w=(), rk=(), wk=(), dma=False):
        op = Op(eng, fn)
        rkeys = set(rk)
        for ap in r:
            rkeys |= _cells(ap)
        wkeys = set(wk)
        for ap in w:
            wkeys |= _cells(ap)
        for k in rkeys:
            st = self.state.get(k)
            if st is None:
                st = self.state[k] = [[], []]
            for p in st[0]:
                self._dep(op, p)
        for k in wkeys:
            st = self.state.get(k)
            if st is None:
                st = self.state[k] = [[], []]
            for p in st[0]:
                self._dep(op, p)
            for p in st[1]:
                self._dep(op, p)
        for k in rkeys:
            if k not in wkeys:
                self.state[k][1].append(op)
        for k in wkeys:
            st = self.state[k]
            if st[1] or not st[0] or st[0][-1].eng != eng or eng not in ("pe", "pool", "sp"):
                st[0] = [op]
            else:
                st[0] = st[0][-3:] + [op]
            st[1] = []
        if dma:
            op.isdma = True
            op.signal = True
            i = self.dma_rr
            self.dma_rr = (i + 1) % self.n_dma_sems
            prev = self.dma_last[i]
            if prev is not None:
                op.deps.append(prev)
            self.dma_last[i] = op
            self.dma_count[i] += 16
            op.sem = ("dma", i)
            op.inc = 16
            op.sigval = self.dma_count[i]
        self.ops[eng].append(op)
        return op

    def emit(self, nc):
        for e in self.ENGS:
            cnt = 0
            for op in self.ops[e]:
                if op.isdma:
                    continue
                op.sem = ("eng", e)
                if op.signal:
                    cnt += 1
                    op.sigval = cnt
        with contextlib.ExitStack() as es:
            sems = {}
            for e in self.ENGS:
                sems[("eng", e)] = es.enter_context(nc.semaphore("s_" + e))
            for i in range(self.n_dma_sems):
                sems[("dma", i)] = es.enter_context(nc.semaphore("d_%d" % i))
            block = es.enter_context(nc.Block())

            def run(ename, eng):
                waited = {}
                for op in self.ops[ename]:
                    need = {}
                    for p in op.deps:
                        if need.get(p.sem, 0) < p.sigval:
                            need[p.sem] = p.sigval
                    for sk, v in need.items():
                        if waited.get(sk, 0) < v:
                            eng.wait_ge(sems[sk], v)
                            waited[sk] = v
                    inst = op.fn(eng)
                    if op.signal:
                        inst.then_inc(sems[op.sem], op.inc)

            @block.tensor
            def _(pe):
                run("pe", pe)

            @block.scalar
            def _(a):
                run("act", a)

            @block.vector
            def _(v):
                run("dve", v)

            @block.gpsimd
            def _(g):
                run("pool", g)

            @block.sync
            def _(s):
                run("sp", s)


def build_nc(n_layers=L, dbg=None):
    nc = bass.Bass("TRN2", target_bir_lowering=False)
    S = Sched()

    def din(name, shape):
        return nc.dram_tensor(name, list(shape), F32, kind="ExternalInput").ap()

    xT = din("xT", [D, NT])
    memT = din("memT", [D, MEM])
    vecs = din("vecs", [128, n_layers * VW])
    bspd = din("bsp", [n_layers, 128, 1024])
    wsTd = din("wsT", [n_layers, 128, 1024])
    wpd = din("wpl", [n_layers, 128, 512])
    cmask = din("cmask", [128, 1])
    ctab = din("ctab", [128, 64])
    w_in = [din("w_in_%d" % li, [28, 128, 2048]) for li in range(n_layers)]
    w_out = [din("w_out_%d" % li, [16, 128, 2048]) for li in range(n_layers)]
    w_q = [din("w_q_%d" % li, [16, 128, 2048]) for li in range(n_layers)]
    w_k = [din("w_k_%d" % li, [16, 128, 2048]) for li in range(n_layers)]
    w_v = [din("w_v_%d" % li, [16, 128, 2048]) for li in range(n_layers)]
    w_o = [din("w_o_%d" % li, [16, 128, 2048]) for li in range(n_layers)]
    w_up = [din("w_up_%d" % li, [64, 128, 2048]) for li in range(n_layers)]
    w_dn = [din("w_dn_%d" % li, [64, 128, 2048]) for li in range(n_layers)]
    yT = nc.dram_tensor("yT", [D, NREAL], F32, kind="ExternalOutput").ap()
    xs = nc.dram_tensor("xs", [D, NT], F32, kind="Internal").ap()

    es = contextlib.ExitStack()
    with es:
        SB = es.enter_context(nc.sbuf_tensor("SB", [128, 45056], F32))
        PS = es.enter_context(nc.psum_tensor("PS", [128, 8, 512], F32))

        def fv(lo, n):
            return SB[:, lo:lo + n]

        def bv(lo, nb):
            return SB[:, lo:lo + nb // 2].bitcast(BF16)

        A0, B10, B20, WR0, T0, C0 = 0, 18432, 27648, 36864, 40960, 43264
        A = fv(A0, 16 * NT).rearrange("p (c t) -> p c t", c=16)
        B1 = bv(B10, 16 * NT).rearrange("p (c t) -> p c t", c=16)
        B2 = bv(B20, 16 * NT).rearrange("p (c t) -> p c t", c=16)
        WR = [bv(WR0 + 1024 * i, 2048) for i in range(4)]
        R = fv(T0, NT)
        SQ = [bv(T0 + 1152, NT), bv(T0 + 1152 + 576, NT)]
        SQF = [fv(T0 + 1152, 576), fv(T0 + 1152 + 576, 576)]
        c = C0
        VEC = fv(c, L * VW); c += L * VW
        WMT = bv(c, 1024).rearrange("p (h t) -> p h t", h=8); c += 512
        WPL = bv(c, 512).rearrange("p (g d) -> p g d", g=4); c += 256
        ONES = bv(c, 128); c += 64
        ONESL = bv(c, 128); c += 64
        MASK = fv(c, 1); c += 4
        TAB = fv(c, 64).rearrange("p (g t) -> p g t", g=4); c += 64
        assert c <= 45056, c
        XSG = [fv(B20, NT), fv(B20 + NT, NT)]
        TMPF = [fv(B20 + 2 * NT, NT), fv(B20 + 3 * NT, NT)]

        STB = [PS[:, i, :] for i in range(3)]
        MMB = [PS[:, 3 + i, :] for i in range(5)]
        mm_rr = [0]

        def mmbank():
            b = MMB[mm_rr[0]]
            mm_rr[0] = (mm_rr[0] + 1) % 5
            return b

        def matmul(out, lhsT, rhs, start, stop):
            S.add("pe", lambda e: e.matmul(out, lhsT=lhsT, rhs=rhs, start=start, stop=stop),
                  r=[lhsT, rhs], w=[out])

        def act(out, in_, func, bias=None, scale=None, eng="act"):
            kw = {}
            r = [in_]
            if bias is not None:
                kw["bias"] = bias
                if not isinstance(bias, (int, float)):
                    r.append(bias)
            if scale is not None:
                kw["scale"] = scale
                if not isinstance(scale, (int, float)):
                    r.append(scale)
            S.add("act", lambda e: e.activation(out=out, in_=in_, func=func, **kw), r=r, w=[out])

        def tt(out, in0, in1, op, eng="dve"):
            S.add(eng, lambda e: e.tensor_tensor(out=out, in0=in0, in1=in1, op=op), r=[in0, in1], w=[out])

        def ts(out, in0, s1, s2, op0, op1=None, eng="dve"):
            r = [in0]
            if not isinstance(s1, (int, float)):
                r.append(s1)
            if s2 is not None and not isinstance(s2, (int, float)):
                r.append(s2)
            if op1 is None:
                S.add(eng, lambda e: e.tensor_scalar(out=out, in0=in0, scalar1=s1, scalar2=None, op0=op0), r=r, w=[out])
            else:
                S.add(eng, lambda e: e.tensor_scalar(out=out, in0=in0, scalar1=s1, scalar2=s2, op0=op0, op1=op1),
                      r=r, w=[out])

        def stt(out, in0, sc, in1, op0, op1):
            r = [in0, in1]
            if not isinstance(sc, (int, float)):
                r.append(sc)
            S.add("dve", lambda e: e.scalar_tensor_tensor(out=out, in0=in0, scalar=sc, in1=in1, op0=op0, op1=op1),
                  r=r, w=[out])

        def dma(q, out, in_, r=(), w=(), rk=(), wk=()):
            return S.add(q, lambda e: e.dma_start(out=out, in_=in_), r=r, w=w, rk=rk, wk=wk, dma=True)

        def memset(ap, val, eng="dve"):
            S.add(eng, lambda e: e.memset(ap, val), w=[ap])

        wr_rr = [0]

        def load_w(src):
            i = wr_rr[0]
            wr_rr[0] = (i + 1) % 4
            slot = WR[i]
            dma("pool", slot, src, w=[slot])
            return slot.rearrange("p (k n) -> p k n", k=KC)

        pending = []

        def flush_pending():
            while pending:
                pending.pop(0)()

        def proj(wsrc, inb, evac, groups=GROUPS):
            wt = load_w(wsrc)
            for gi, (t0, n) in enumerate(groups):
                ps = mmbank()[:, :n]
                for kc in range(KC):
                    matmul(ps, wt[:, kc, :], inb[:, kc, t0:t0 + n], kc == 0, kc == KC - 1)
                if gi == 1:
                    flush_pending()
                evac(gi, t0, n, ps)

        def rstd_from(ps, out, eps):
            act(out, ps, AF.Ln, bias=EPSC[eps], scale=1.0 / D)
            act(out, out, AF.Exp, scale=-0.5)

        IDENT = bv(c, 128); c += 64
        EPS_R = fv(c, 1); c += 4
        EPS_L = fv(c, 1); c += 4
        assert c <= 45056
        EPSC = {RMS_EPS: EPS_R, LN_EPS: EPS_L}
        memset(EPS_R, RMS_EPS)
        memset(EPS_L, LN_EPS)
        memset(ONES, 1.0)
        memset(ONESL, 1.0 / 512.0)
        memset(IDENT, 1.0)
        S.add("pool", lambda e: e.affine_select(out=IDENT, in_=IDENT, pattern=[[1, 128]], compare_op=ALU.is_equal,
                                                fill=0.0, base=0, channel_multiplier=-1), r=[IDENT], w=[IDENT])
        dma("sp", VEC[:, 0:n_layers * VW], vecs, w=[VEC])
        dma("sp", MASK, cmask, w=[MASK])
        dma("sp", TAB.rearrange("p g t -> p (g t)"), ctab, w=[TAB])

        def vcol(l, base, i):
            return VEC[:, l * VW + base + i: l * VW + base + i + 1]

        slot_rr = [0]

        def sqslot(n):
            k = slot_rr[0]
            slot_rr[0] += 1
            if n <= 384:
                return SQ[k % 2][:, (k // 2 % 3) * 384:(k // 2 % 3) * 384 + n]
            return SQ[k % 2][:, (k // 2 % 2) * 512:(k // 2 % 2) * 512 + n]

        def prenorm(l, vbase, first, groups):
            lo = groups[0][0]
            if first:
                for cch in range(KC):
                    dma("sp", A[:, cch, :], xT[cch * 128:(cch + 1) * 128, :], w=[A[:, cch, :]])
            for cch in range(KC):
                sq = SQ[cch % 2]
                act(sq[:, lo:NT], A[:, cch, lo:NT], AF.Square)
                for gi, (t0, n) in enumerate(groups):
                    matmul(STB[gi][:, :n], ONES, sq[:, t0:t0 + n], cch == 0, cch == KC - 1)
            for gi, (t0, n) in enumerate(groups):
                rstd_from(STB[gi][:, :n], MMB[gi][:, :n], RMS_EPS)
            for cch in range(KC):
                for gi, (t0, n) in enumerate(groups):
                    stt(B1[:, cch, t0:t0 + n], A[:, cch, t0:t0 + n], vcol(l, vbase, cch), MMB[gi][:, :n],
                        ALU.mult, ALU.mult)

        RING = [fv(B10 + i * NT, NT) for i in range(8)]

        def postadd(l, vbase, src, xsrc_is_input, to_output, groups, inplace):
            lo = groups[0][0]
            for gi, (t0, n) in enumerate(groups):
                rstd_from(STB[gi][:, :n], MMB[gi][:, :n], RMS_EPS)
            def xbuf(j):
                return A[:, j, :] if inplace else RING[j % 8]

            def xload(j):
                rows = slice(j * 128, (j + 1) * 128)
                xg = xbuf(j)[:, lo:NT]
                if xsrc_is_input:
                    dma("sp", xg, xT[rows, lo:NT], w=[xg])
                else:
                    dma("sp", xg, xs[rows, lo:NT], w=[xg], rk=[("xs", j)])

            npre = KC if inplace else 8
            for j in range(npre):
                xload(j)
            k = 0
            for j in range(KC):
                rows = slice(j * 128, (j + 1) * 128)
                xfull = xbuf(j)
                for gi, (t0, n) in enumerate(groups):
                    tp = MMB[3 + k % 2][:, :n]
                    k += 1
                    stt(tp, src[:, j, t0:t0 + n], vcol(l, vbase, j), MMB[gi][:, :n], ALU.mult, ALU.mult)
                    tt(A[:, j, t0:t0 + n], tp, xfull[:, t0:t0 + n], ALU.add)
                if to_output:
                    outd.append(dma("sp", yT[rows, :], A[:, j, HALO:NT], r=[A[:, j, HALO:NT]]))
                else:
                    dma("sp", xs[rows, lo:NT], A[:, j, lo:NT], r=[A[:, j, lo:NT]], wk=[("xs", j)])
                if j + npre < KC:
                    xload(j + npre)

        def evac_h2(j):
            def ev(gi, t0, n, ps):
                act(B1[:, j, t0:t0 + n], ps, AF.Copy)
                slot = sqslot(n)
                act(slot, ps, AF.Square)
                pending.append(lambda: matmul(STB[gi][:, :n], ONES, slot, j == 0, j == KC - 1))
            return ev

        outd = []

        GA = [(96, 352), (448, 352), (800, 352)]
        GB = [(128, 512), (640, 512)]
        for l in range(n_layers):
            last = (l == n_layers - 1)
            GM = GROUPS if l == 0 else GA
            GP = GB if last else GA
            GU = GROUPS if l == 0 else GB
            tblocks = [(0, 4), (4, 4), (8, 1)] if l == 0 else [(1, 4), (5, 4)]
            tl0 = tblocks[0][0]
            prenorm(l, V_MIXPRE, first=(l == 0), groups=GM)
            BSP = fv(A0 + 17408, 1024).rearrange("p (h t) -> p h t", h=8)
            dma("sp", BSP.rearrange("p h t -> p (h t)"), bspd[l], w=[BSP])
            dma("pool", WMT.rearrange("p h t -> p (h t)"), wsTd[l], w=[WMT])
            S.add("pool", lambda e: e.affine_select(out=WMT, in_=WMT, pattern=[[0, 8], [1, 128]],
                                                    compare_op=ALU.is_ge, fill=0.0, base=0, channel_multiplier=-1),
                  r=[WMT], w=[WMT])
            dma("pool", WPL.rearrange("p g d -> p (g d)"), wpd[l], w=[WPL])

            P = fv(A0 + 12416, 4 * 1168).rearrange("p (g t) -> p g t", g=4)
            PL = bv(A0 + 10512, NT)
            S1 = fv(A0 + 11088, 1168)
            S2 = fv(A0 + 9344, 1168)
            memset(P[:, :, 0:16], 0.0)
            for g in range(4):
                def ev(gi, t0, n, ps, g=g):
                    act(P[:, g, 16 + t0:16 + t0 + n], ps, AF.Copy)
                proj(w_in[l][16 + g], B1, ev, groups=GM)
            def pool_group(g):
                if True:
                    w = 2 << g
                    cur = P[:, g, :]
                    bufs = [S1, S2]
                    for s in range(g + 1):
                        sh = 1 << s
                        nxt = bufs[s % 2]
                        tt(nxt[:, sh:1168], cur[:, sh:1168], cur[:, 0:1168 - sh], ALU.add)
                        cur = nxt
                    stt(PL, cur[:, 16:1168], 1.0 / w, P[:, g, 16:1168], ALU.mult, ALU.subtract)
                    t1 = SQF[0][:, 0:16]
                    tt(t1, cur[:, 16 + HALO:16 + HALO + 16], TAB[:, g, :], ALU.mult)
                    tt(PL[:, HALO:HALO + 16], t1, P[:, g, 16 + HALO:16 + HALO + 16], ALU.subtract)
                    for gi, (t0, n) in enumerate(GM):
                        ps = mmbank()[:, :n]
                        matmul(ps, WPL[:, g, :], PL[:, t0:t0 + n], True, True)
                        ts(B2[:, 8 + g, t0:t0 + n], ps, vcol(l, V_SPOOL, g), None, ALU.mult)

            H = bv(A0, 4 * 1184).rearrange("p (g t) -> p g t", g=4)
            DG = bv(A0 + 2368, 31 * 128).rearrange("p (j c) -> p j c", j=31)
            CO = fv(A0 + 4736, 4 * NT).rearrange("p (g t) -> p g t", g=4)
            SIG = fv(A0 + 9344, NT)
            LT = [fv(A0 + 10496 + 384 * i, 384) for i in range(5)]
            CS4 = [bv(A0 + 9344 + 192 * i, 384) for i in range(4)]
            memset(H[:, :, 0:32], 0.0)
            for cc in range(4):
                def evg(gi, t0, n, ps):
                    act(SIG[:, t0:t0 + n], ps, AF.Sigmoid)
                proj(w_in[l][24 + cc], B1, evg, groups=GM)

                def evv(gi, t0, n, ps, cc=cc):
                    tt(H[:, cc, 32 + t0:32 + t0 + n], ps, SIG[:, t0:t0 + n], ALU.mult)
                proj(w_in[l][20 + cc], B1, evv, groups=GM)
            ts(H[:, :, 32:32 + HALO], H[:, :, 32:32 + HALO], MASK, None, ALU.mult)
            ts(P[:, :, 16:16 + HALO], P[:, :, 16:16 + HALO], MASK, None, ALU.mult)

            def conv_diag(cc):
                for j in range(31):
                    ts(DG[:, j, :], IDENT, vcol(l, V_WDW, cc * 31 + j), None, ALU.mult)
                for gi, (t0, n) in enumerate(GM):
                    ps = mmbank()[:, :n]
                    for j in range(31):
                        matmul(ps, DG[:, j, :], H[:, cc, 2 + t0 + j:2 + t0 + j + n], j == 0, j == 30)
                    act(CO[:, cc, t0:t0 + n], ps, AF.Identity, bias=vcol(l, V_BDW, cc))

            for i in range(4):
                conv_diag(i)
                pool_group(i)

            def conv_ln(gis):
              for gi in gis:
                t0, n = GM[gi]
                psM = mmbank()[:, :n]
                psQ = mmbank()[:, :n]
                for cc in range(4):
                    cb = SQ[0][:, cc % 3 * 384:cc % 3 * 384 + n] if cc < 3 else SQ[1][:, 0:n]
                    cs = CS4[cc][:, :n]
                    act(cb, CO[:, cc, t0:t0 + n], AF.Copy)
                    act(cs, CO[:, cc, t0:t0 + n], AF.Square)
                    matmul(psM, ONESL, cb, cc == 0, cc == 3)
                    matmul(psQ, ONESL, cs, cc == 0, cc == 3)
                mean, msq, var, t1a, t1b = [x[:, :n] for x in LT]
                act(mean, psM, AF.Copy)
                act(msq, psM, AF.Square)
                tt(var, psQ, msq, ALU.subtract)
                act(var, var, AF.Ln, bias=EPS_L, scale=1.0)
                act(var, var, AF.Exp, scale=-0.5)
                for cc in range(4):
                    t1 = (t1a, t1b)[cc % 2]
                    tt(t1, CO[:, cc, t0:t0 + n], mean, ALU.subtract)
                    tt(t1, t1, var, ALU.mult)
                    act(B2[:, 12 + cc, t0:t0 + n], t1, AF.Silu, bias=vcol(l, V_LNB, cc), scale=vcol(l, V_LNG, cc))

            G0 = A0 + 12416
            UT = [bv(G0, NT), bv(G0 + 576, NT)]
            VG = [fv(G0 + 1152, NT), fv(G0 + 2304, NT)]
            VN = [bv(G0 + 3456, NT), bv(G0 + 3456 + 576, NT)]
            BS = fv(G0 + 4608, 54).rearrange("p (t s) -> p t s", t=9)
            MV = fv(G0 + 4608 + 64, 18).rearrange("p (t s) -> p t s", t=9)
            RSD = fv(G0 + 4608 + 96, 9)
            def gmlp_proj(h):
                ut = UT[h % 2]
                vg = VG[h % 2]

                def evu(gi, t0, n, ps, ut=ut):
                    act(ut[:, t0:t0 + n], ps, AF.Gelu_apprx_tanh)
                proj(w_in[l][h], B1, evu, groups=GU)
                wt = load_w(w_in[l][8 + h])
                for (tb0, ntb) in tblocks:
                    ps = mmbank()
                    for i in range(ntb):
                        ti = tb0 + i
                        for kc in range(KC):
                            matmul(ps[:, i * 128:(i + 1) * 128], B1[:, kc, ti * 128:(ti + 1) * 128], wt[:, kc, :],
                                   kc == 0, kc == KC - 1)
                    act(vg[:, tb0 * 128:(tb0 + ntb) * 128], ps[:, :ntb * 128], AF.Gelu_apprx_tanh)

            def gmlp_stats(h):
                vg = VG[h % 2]
                vn = VN[h % 2]
                for ti in range(tl0, NTILE):
                    S.add("dve", lambda e, ti=ti, vg=vg: e.bn_stats(out=BS[:, ti, :], in_=vg[:, ti * 128:(ti + 1) * 128]),
                          r=[vg[:, ti * 128:(ti + 1) * 128]], w=[BS[:, ti, :]])
                    S.add("dve", lambda e, ti=ti: e.bn_aggr(out=MV[:, ti, :], in_=BS[:, ti, :]),
                          r=[BS[:, ti, :]], w=[MV[:, ti, :]])
                act(RSD[:, tl0:NTILE], MV[:, tl0:NTILE, 1], AF.Ln, bias=EPS_L, scale=1.0)
                act(RSD[:, tl0:NTILE], RSD[:, tl0:NTILE], AF.Exp, scale=-0.5)
                for ti in range(tl0, NTILE):
                    ts(vn[:, ti * 128:(ti + 1) * 128], vg[:, ti * 128:(ti + 1) * 128], MV[:, ti, 0:1], RSD[:, ti:ti + 1],
                       ALU.subtract, ALU.mult)

            def gmlp_mix(h):
                ut = UT[h % 2]
                vn = VN[h % 2]
                for bi, (tb0, ntb) in enumerate(tblocks):
                    ps = mmbank()
                    for i in range(ntb):
                        ti = tb0 + i
                        matmul(ps[:, i * 128:(i + 1) * 128], vn[:, ti * 128:(ti + 1) * 128], WMT[:, h, :], True, True)
                    t1 = SQF[(h * 3 + bi) % 2][:, 0:ntb * 128]
                    for i in range(ntb):
                        stt(t1[:, i * 128:(i + 1) * 128], ps[:, i * 128:(i + 1) * 128], vcol(l, V_GV, h), BSP[:, h, :],
                            ALU.mult, ALU.add)
                    tt(B2[:, h, tb0 * 128:(tb0 + ntb) * 128], t1, ut[:, tb0 * 128:(tb0 + ntb) * 128], ALU.mult)

            gmlp_proj(0)
            conv_ln([0])
            gmlp_proj(1)
            conv_ln([1, 2])
            for h in range(8):
                gmlp_stats(h)
                if 2 <= h + 1 < 8:
                    gmlp_proj(h + 1)
                gmlp_mix(h)

            for j in range(KC):
                proj(w_out[l][j], B2, evac_h2(j), groups=GP)
            flush_pending()
            postadd(l, V_MIXPOST, B1, xsrc_is_input=(l == 0), to_output=False, groups=GP, inplace=True)
            if dbg == "mix" and l == n_layers - 1:
                break

            prenorm(l, V_XAPRE, first=False, groups=GP)
            MT = fv(A0, 16 * MEM).rearrange("p (c m) -> p c m", c=16)
            MN = bv(A0 + 4096, 16 * MEM).rearrange("p (c m) -> p c m", c=16)
            KT = bv(A0 + 6144, 16 * MEM).rearrange("p (c m) -> p c m", c=16)
            VV = bv(A0 + 8192, 2 * D).rearrange("p (m d) -> p m d", m=2)
            QH = [bv(A0 + 10240, 4 * NT).rearrange("p (c t) -> p c t", c=4),
                  bv(A0 + 12544, 4 * NT).rearrange("p (c t) -> p c t", c=4)]
            EE = [bv(A0 + 14848, 1024).rearrange("p (m t) -> p m t", m=2),
                  bv(A0 + 15360, 1024).rearrange("p (m t) -> p m t", m=2)]
            RINV = [fv(A0 + 15872, 512), fv(A0 + 16384, 512)]
            dma("sp", MT, memT.rearrange("(c p) m -> p c m", p=128), w=[MT])
            psA = mmbank()[:, :MEM]
            for cch in range(KC):
                sq = SQ[cch % 2][:, 0:MEM]
                act(sq, MT[:, cch, :], AF.Square)
                matmul(psA, ONES, sq, cch == 0, cch == KC - 1)
            RM = R[:, 0:MEM]
            rstd_from(psA, RM, RMS_EPS)
            for cch in range(KC):
                stt(MN[:, cch, :], MT[:, cch, :], vcol(l, V_MEM, cch), RM, ALU.mult, ALU.mult)
            for j in range(KC):
                def evk(gi, t0, n, ps, j=j):
                    act(KT[:, j, :], ps, AF.Copy)
                proj(w_k[l][j], MN, evk, groups=[(0, MEM)])
            for j in range(KC):
                wt = load_w(w_v[l][j])
                ps = mmbank()
                for mt in range(2):
                    for kc in range(KC):
                        matmul(ps[:, mt * 128:(mt + 1) * 128], MN[:, kc, mt * 128:(mt + 1) * 128], wt[:, kc, :],
                               kc == 0, kc == KC - 1)
                act(VV[:, :, j * 128:(j + 1) * 128], ps[:, 0:256].rearrange("p (m d) -> p m d", m=2), AF.Copy)
            scale = 512.0 ** -0.5
            it = 0

            def q_proj(h):
                qh = QH[h % 2]
                for dc in range(4):
                    def evq(gi, t0, n, ps, dc=dc, qh=qh):
                        act(qh[:, dc, t0:t0 + n], ps, AF.Copy)
                    proj(w_q[l][h * 4 + dc], B1, evq, groups=GP)

            def s_part(h, gi, itn):
                t0, n = GP[gi]
                qh = QH[h % 2]
                ee = EE[itn % 2]
                for mt in range(2):
                    psS = mmbank()[:, :n]
                    for dc in range(4):
                        matmul(psS, KT[:, h * 4 + dc, mt * 128:(mt + 1) * 128], qh[:, dc, t0:t0 + n], dc == 0, dc == 3)
                    act(ee[:, mt, :n], psS, AF.Exp, scale=scale)

            def pv_part(h, gi, itn):
                t0, n = GP[gi]
                ee = EE[itn % 2]
                rinv = RINV[itn % 2][:, :n]
                psD = mmbank()[:, :n]
                for mt in range(2):
                    matmul(psD, ONES, ee[:, mt, :n], mt == 0, mt == 1)
                S.add("dve", lambda e, rinv=rinv, psD=psD: e.reciprocal(out=rinv, in_=psD), r=[psD], w=[rinv])
                for dc in range(4):
                    psO = mmbank()[:, :n]
                    for mt in range(2):
                        matmul(psO, VV[:, mt, h * 512 + dc * 128:h * 512 + (dc + 1) * 128], ee[:, mt, :n], mt == 0, mt == 1)
                    tt(B2[:, h * 4 + dc, t0:t0 + n], psO, rinv, ALU.mult)

            q_proj(0)
            for h in range(4):
                if h + 1 < 4:
                    q_proj(h + 1)
                s_part(h, 0, it)
                for gi in range(len(GP)):
                    if gi + 1 < len(GP):
                        s_part(h, gi + 1, it + 1)
                    pv_part(h, gi, it)
                    it += 1
            for j in range(KC):
                proj(w_o[l][j], B2, evac_h2(j), groups=GP)
            flush_pending()
            postadd(l, V_XAPOST, B1, xsrc_is_input=False, to_output=False, groups=GP, inplace=True)
            if dbg == "xa" and l == n_layers - 1:
                break

            prenorm(l, V_FFNPRE, first=False, groups=GP)
            for hb in range(4):
                for hc in range(KC):
                    def evr(gi, t0, n, ps, hc=hc):
                        rt = sqslot(n)
                        act(rt, ps, AF.Relu)
                        stt(B2[:, hc, t0:t0 + n], ps, 0.0, rt, ALU.max, ALU.mult)
                    proj(w_up[l][hb * 16 + hc], B1, evr, groups=GP)
                for j in range(KC):
                    def evd(gi, t0, n, ps, j=j, hb=hb):
                        if hb == 0:
                            act(A[:, j, t0:t0 + n], ps, AF.Copy)
                        else:
                            tt(A[:, j, t0:t0 + n], ps, A[:, j, t0:t0 + n], ALU.add)
                    proj(w_dn[l][hb * 16 + j], B2, evd, groups=GP)
            lo_p = GP[0][0]
            for j in range(KC):
                sq = SQ[j % 2]
                act(sq[:, lo_p:NT], A[:, j, lo_p:NT], AF.Square)
                for gi, (t0, n) in enumerate(GP):
                    matmul(STB[gi][:, :n], ONES, sq[:, t0:t0 + n], j == 0, j == KC - 1)
            postadd(l, V_FFNPOST, A, xsrc_is_input=False, to_output=last, groups=GP, inplace=False)

        if not outd:
            for j in range(KC):
                rows = slice(j * 128, (j + 1) * 128)
                outd.append(dma("sp", yT[rows, :], A[:, j, HALO:NT], r=[A[:, j, HALO:NT]]))
        fin = S.add("sp", lambda e: e.nop(), r=(), w=())
        for o in outd:
            fin.deps.append(o)
        S.emit(nc)
    return nc


def _tile_w(w):
    K, C = w.shape
    nb = K // 2048
    t = w.reshape(nb, 16, 128, C // 128, 128).transpose(0, 3, 2, 1, 4)
    return np.ascontiguousarray(t).reshape(nb * (C // 128), 128, 2048)


def _prep_shared(inp, layers=(0, 1)):
    f = np.float32
    sh = {}
    for name, key in (("w_in", "w_in"), ("w_out", "w_out"), ("w_q", "w_q"), ("w_k", "w_k"), ("w_v", "w_v"),
                      ("w_o", "w_o"), ("w_up", "w_up"), ("w_dn", "w_down")):
        w = np.asarray(inp[key], dtype=f)
        for li, l in enumerate(layers):
            sh["%s_%d" % (name, li)] = _tile_w(w[l])
    NL = len(layers)
    vec = np.zeros((128, NL * VW), f)
    for li, l in enumerate(layers):
        b = li * VW
        for base, key in ((V_MIXPRE, "norm_mix_pre"), (V_MIXPOST, "norm_mix_post"), (V_XAPRE, "norm_xattn_pre"),
                          (V_MEM, "norm_mem"), (V_XAPOST, "norm_xattn_post"), (V_FFNPRE, "norm_ffn_pre"),
                          (V_FFNPOST, "norm_ffn_post")):
            vec[:, b + base:b + base + 16] = np.asarray(inp[key], f)[l].reshape(16, 128).T
        vec[:, b + V_GV:b + V_GV + 8] = np.asarray(inp["gmlp_v_gain"], f)[l].T
        for base, key in ((V_SPOOL, "s_pool"), (V_BDW, "b_dw"), (V_LNG, "conv_ln_g"), (V_LNB, "conv_ln_b")):
            vec[:, b + base:b + base + 4] = np.asarray(inp[key], f)[l].reshape(4, 128).T
        vec[:, b + V_WDW:b + V_WDW + 124] = np.asarray(inp["w_dw"], f)[l].reshape(31, 4, 128).transpose(2, 1, 0).reshape(128, 124)
    sh["vecs"] = vec
    lay = list(layers)
    bs = np.asarray(inp["b_spatial"], f)[lay]
    sh["bsp"] = np.ascontiguousarray(np.broadcast_to(bs.reshape(NL, 1, 1024), (NL, 128, 1024)))
    ws = np.asarray(inp["w_spatial"], f)[lay]
    sh["wsT"] = np.ascontiguousarray(ws.transpose(0, 3, 1, 2)).reshape(NL, 128, 1024)
    wp = np.asarray(inp["w_pool"], f)[lay]
    sh["wpl"] = np.ascontiguousarray(wp.transpose(0, 2, 1, 3)).reshape(NL, 128, 512)
    return sh


def _prep_core(inp, core, x=None):
    f = np.float32
    x = np.asarray(inp["x"], f) if x is None else x
    mem = np.asarray(inp["mem"], f)
    b, half = core // 2, core % 2
    s0 = half * NREAL
    xt = np.zeros((D, NT), f)
    if half == 1:
        xt[:, :] = x[b, s0 - HALO:s0 + NREAL, :].T
    else:
        xt[:, HALO:] = x[b, 0:NREAL, :].T
    d = {"xT": xt, "memT": np.ascontiguousarray(mem[b].T)}
    d["cmask"] = np.full((128, 1), 1.0 if half == 1 else 0.0, f)
    tab = np.zeros((4, 16), f)
    for g, w in enumerate((2, 4, 8, 16)):
        for t in range(16):
            cnt = float(w) if half == 1 else float(min(t + 1, w))
            tab[g, t] = 1.0 / cnt
    d["ctab"] = np.ascontiguousarray(np.broadcast_to(tab.reshape(1, 64), (128, 64)))
    return d


_NC_CACHE = {}

FUSED = True


def _gather(res):
    out = np.zeros((4, 2048, D), np.float32)
    for core in range(8):
        b, half = core // 2, core % 2
        out[b, half * NREAL:(half + 1) * NREAL, :] = res.results[core]["yT"].T
    return out


def kernel(**inputs):
    if FUSED:
        sh = _prep_shared(inputs, (0, 1))
        in_maps = []
        for core in range(8):
            d = dict(sh)
            d.update(_prep_core(inputs, core))
            in_maps.append(d)
        if "nc2" not in _NC_CACHE:
            _NC_CACHE["nc2"] = build_nc(2)
        res = run_bass_kernel_spmd(_NC_CACHE["nc2"], in_maps, core_ids=list(range(8)))
        return _gather(res)
    if "nc1" not in _NC_CACHE:
        _NC_CACHE["nc1"] = build_nc(1)
    x = np.asarray(inputs["x"], np.float32)
    for l in range(L):
        sh = _prep_shared(inputs, (l,))
        in_maps = []
        for core in range(8):
            d = dict(sh)
            d.update(_prep_core(inputs, core, x=x))
            in_maps.append(d)
        res = run_bass_kernel_spmd(_NC_CACHE["nc1"], in_maps, core_ids=list(range(8)))
        x = _gather(res)
    return x
```
